# Optimizing a Trainium2 kernel written in Bass

```python
import jax, jax.numpy as jnp
from jax import lax
import numpy as np

D_MODEL = 1024
BATCH = 4
SEQ = 4096
DEPTH = 4

DN_HEADS = 8
DN_HEAD_DIM = 128
DN_WIDTH = DN_HEADS * DN_HEAD_DIM
DN_CONV = 4
DN_CHUNK = 64
SWA_Q_HEADS = 16
SWA_KV_HEADS = 2
SWA_HEAD_DIM = 64
SWA_GROUP = SWA_Q_HEADS // SWA_KV_HEADS
SWA_WIDTH = SWA_Q_HEADS * SWA_HEAD_DIM
SWA_KV_WIDTH = SWA_KV_HEADS * SWA_HEAD_DIM
WINDOW = 128
SWA_BLOCK = 128
ROPE_THETA = 500000.0
ROPE_DIM = SWA_HEAD_DIM // 4
D_FF = 2816
FFN_CONV = 3
EPS = 1e-6
IN_SIZES = (3 * DN_WIDTH, DN_WIDTH, DN_HEADS, DN_HEADS, SWA_WIDTH, SWA_KV_WIDTH, SWA_KV_WIDTH, D_MODEL, D_MODEL)
IN_TOTAL = 4 * DN_WIDTH + 2 * DN_HEADS + SWA_WIDTH + 2 * SWA_KV_WIDTH + 2 * D_MODEL

kernel_name = "hybrid_gdn_swa_sink_convffn_adaln"


def _split_columns(t, sizes):
    idx, acc = [], 0
    for s in sizes[:-1]:
        acc += s
        idx.append(acc)
    return jnp.split(t, idx, axis=-1)


def rms_norm(x, w):
    xf = x.astype(jnp.float32)
    y = xf * lax.rsqrt(jnp.mean(xf * xf, axis=-1, keepdims=True) + EPS)
    return (y * w.astype(jnp.float32)).astype(x.dtype)


def l2_norm(x):
    xf = x.astype(jnp.float32)
    return (xf * lax.rsqrt(jnp.sum(xf * xf, axis=-1, keepdims=True) + EPS)).astype(x.dtype)


def causal_dwconv(x, w):
    K, C = w.shape
    return lax.conv_general_dilated(
        x, w[:, None, :].astype(x.dtype), window_strides=(1,), padding=[(K - 1, 0)],
        dimension_numbers=('NWC', 'WIO', 'NWC'), feature_group_count=C)


def partial_rope(x, pos):
    half = ROPE_DIM // 2
    inv = jnp.power(ROPE_THETA, -jnp.arange(half, dtype=jnp.float32) / half)
    ang = pos.astype(jnp.float32)[..., None] * inv
    cos, sin = jnp.cos(ang)[:, :, None, :], jnp.sin(ang)[:, :, None, :]
    xr = x[..., :ROPE_DIM].astype(jnp.float32)
    x1, x2 = xr[..., :half], xr[..., half:]
    rot = jnp.concatenate([x1 * cos - x2 * sin, x2 * cos + x1 * sin], axis=-1).astype(x.dtype)
    return jnp.concatenate([rot, x[..., ROPE_DIM:]], axis=-1)


def gated_delta_rule_chunked(q, k, v, g, beta):
    B, T, H, Dk = q.shape
    Dv = v.shape[-1]
    C = DN_CHUNK
    N = T // C
    f32 = jnp.float32

    def chunks(t):
        return t.astype(f32).reshape(B, N, C, H, -1).transpose(0, 3, 1, 2, 4)

    qc = chunks(q) * (Dk ** -0.5)
    kc, vc = chunks(k), chunks(v)
    gc = jnp.cumsum(g.astype(f32).reshape(B, N, C, H).transpose(0, 3, 1, 2), axis=-1)
    bc = beta.astype(f32).reshape(B, N, C, H).transpose(0, 3, 1, 2)
    tri_incl = jnp.tril(jnp.ones((C, C), dtype=bool))
    tri_strict = jnp.tril(jnp.ones((C, C), dtype=bool), -1)
    decay = jnp.exp(jnp.where(tri_incl, gc[..., :, None] - gc[..., None, :], -jnp.inf))
    k_beta = kc * bc[..., None]
    L = jnp.where(tri_strict, jnp.einsum('bhnid,bhnjd->bhnij', k_beta, kc) * decay, 0.0)
    A = L + jnp.eye(C, dtype=f32)
    rhs = jnp.concatenate([vc * bc[..., None], k_beta * jnp.exp(gc)[..., None]], axis=-1)
    sol = lax.linalg.triangular_solve(A, rhs, left_side=True, lower=True, unit_diagonal=True)
    u, w = sol[..., :Dv], sol[..., Dv:]
    qk = jnp.einsum('bhnid,bhnjd->bhnij', qc, kc) * decay
    q_dec = qc * jnp.exp(gc)[..., None]
    g_last = gc[..., -1]
    k_dec = kc * jnp.exp(g_last[..., None] - gc)[..., None]

    def step(S, xs):
        qd, kd, u_i, w_i, qk_i, gl = xs
        v_new = u_i - jnp.einsum('bhcd,bhde->bhce', w_i, S)
        o = jnp.einsum('bhcd,bhde->bhce', qd, S) + jnp.einsum('bhij,bhje->bhie', qk_i, v_new)
        S = S * jnp.exp(gl)[..., None, None] + jnp.einsum('bhcd,bhce->bhde', kd, v_new)
        return S, o

    xs = tuple(jnp.moveaxis(t, 2, 0) for t in (q_dec, k_dec, u, w, qk, g_last))
    S0 = jnp.zeros((B, H, Dk, Dv), f32)
    _, o = lax.scan(step, S0, xs)
    return o.transpose(1, 0, 3, 2, 4).reshape(B, T, H, Dv).astype(v.dtype)


def swa_sink_attention(q, k, v, sinks):
    B, T, Hq, D = q.shape
    nb = T // SWA_BLOCK
    f32 = jnp.float32
    qb = q.astype(f32).reshape(B, nb, SWA_BLOCK, SWA_KV_HEADS, SWA_GROUP, D)

    def band(t):
        tb = t.astype(f32).reshape(B, nb, SWA_BLOCK, SWA_KV_HEADS, D)
        prev = jnp.pad(tb, ((0, 0), (1, 0), (0, 0), (0, 0), (0, 0)))[:, :-1]
        return jnp.concatenate([prev, tb], axis=2)

    kb, vb = band(k), band(v)
    s = jnp.einsum('bnqhgd,bnkhd->bnhgqk', qb, kb) * (D ** -0.5)
    qi = jnp.arange(SWA_BLOCK)[:, None]
    kj = jnp.arange(2 * SWA_BLOCK)[None, :]
    rel = qi + SWA_BLOCK - kj
    blk = jnp.arange(nb)[:, None, None]
    mask = (rel >= 0) & (rel < WINDOW) & (blk * SWA_BLOCK + kj >= SWA_BLOCK)
    s = jnp.where(mask[None, :, None, None], s, -jnp.inf)
    sink = sinks.astype(f32).reshape(1, 1, SWA_KV_HEADS, SWA_GROUP, 1, 1)
    m = jnp.maximum(jnp.max(s, axis=-1, keepdims=True), sink)
    p = jnp.exp(s - m)
    denom = jnp.sum(p, axis=-1, keepdims=True) + jnp.exp(sink - m)
    o = jnp.einsum('bnhgqk,bnkhd->bnqhgd', p / denom, vb)
    return o.reshape(B, T, Hq * D).astype(q.dtype)


def setup_inputs(seed: int = 0) -> dict:
    key = jax.random.key(seed)
    ks = jax.random.split(key, 24)
    nrm = jax.random.normal
    f32 = jnp.float32
    Lr = DEPTH
    x = nrm(ks[0], (BATCH, SEQ, D_MODEL), f32)
    c = nrm(ks[1], (BATCH, D_MODEL), f32)
    positions = (jax.random.randint(ks[2], (BATCH, 1), 0, 2048, dtype=jnp.int32)
                 + jnp.arange(SEQ, dtype=jnp.int32)[None, :])
    w_ada = nrm(ks[3], (Lr, D_MODEL, 6 * D_MODEL), f32) * D_MODEL ** -0.5
    b_ada = nrm(ks[4], (Lr, 6 * D_MODEL), f32) * 0.02
    norm_mix = 1.0 + 0.02 * nrm(ks[5], (Lr, D_MODEL), f32)
    w_in = nrm(ks[6], (Lr, D_MODEL, IN_TOTAL), f32) * D_MODEL ** -0.5
    dn_conv = nrm(ks[7], (Lr, DN_CONV, 3 * DN_WIDTH), f32) * DN_CONV ** -0.5
    dn_a_log = jnp.log(jax.random.uniform(ks[8], (Lr, DN_HEADS), f32, 1.0, 16.0))
    dt = jnp.exp(jax.random.uniform(ks[9], (Lr, DN_HEADS), f32, np.log(1e-3), np.log(1e-1)))
    dn_dt_bias = jnp.log(jnp.expm1(dt))
    dn_norm = 1.0 + 0.02 * nrm(ks[10], (Lr, DN_HEAD_DIM), f32)
    w_dn_out = nrm(ks[11], (Lr, DN_WIDTH, D_MODEL), f32) * DN_WIDTH ** -0.5
    swa_q_norm = 1.0 + 0.02 * nrm(ks[12], (Lr, SWA_HEAD_DIM), f32)
    swa_k_norm = 1.0 + 0.02 * nrm(ks[13], (Lr, SWA_HEAD_DIM), f32)
    swa_sinks = nrm(ks[14], (Lr, SWA_Q_HEADS), f32)
    w_swa_out = nrm(ks[15], (Lr, SWA_WIDTH, D_MODEL), f32) * SWA_WIDTH ** -0.5
    w_o = nrm(ks[16], (Lr, D_MODEL, D_MODEL), f32) * D_MODEL ** -0.5
    norm_ffn = 1.0 + 0.02 * nrm(ks[17], (Lr, D_MODEL), f32)
    w_up = nrm(ks[18], (Lr, D_MODEL, 2 * D_FF), f32) * D_MODEL ** -0.5
    ffn_conv = nrm(ks[19], (Lr, FFN_CONV, D_FF), f32) * FFN_CONV ** -0.5
    ffn_conv_b = nrm(ks[20], (Lr, D_FF), f32) * 0.02
    w_down = nrm(ks[21], (Lr, D_FF, D_MODEL), f32) * D_FF ** -0.5
    return {"x": x, "c": c, "positions": positions, "w_ada": w_ada, "b_ada": b_ada,
            "norm_mix": norm_mix, "w_in": w_in, "dn_conv": dn_conv, "dn_a_log": dn_a_log,
            "dn_dt_bias": dn_dt_bias, "dn_norm": dn_norm, "w_dn_out": w_dn_out,
            "swa_q_norm": swa_q_norm, "swa_k_norm": swa_k_norm, "swa_sinks": swa_sinks,
            "w_swa_out": w_swa_out, "w_o": w_o, "norm_ffn": norm_ffn, "w_up": w_up,
            "ffn_conv": ffn_conv, "ffn_conv_b": ffn_conv_b, "w_down": w_down}


def reference(x, c, positions, w_ada, b_ada, norm_mix, w_in, dn_conv, dn_a_log, dn_dt_bias,
              dn_norm, w_dn_out, swa_q_norm, swa_k_norm, swa_sinks, w_swa_out, w_o,
              norm_ffn, w_up, ffn_conv, ffn_conv_b, w_down):
    B, T, _ = x.shape
    c_act = jax.nn.silu(c)
    for l in range(DEPTH):
        mod = c_act @ w_ada[l] + b_ada[l]
        sh1, sc1, gt1, sh2, sc2, gt2 = [m[:, None, :] for m in jnp.split(mod, 6, axis=-1)]

        h = rms_norm(x, norm_mix[l]) * (1.0 + sc1) + sh1
        proj = h @ w_in[l]
        dn_qkv, dn_z, dn_a, dn_b, sw_q, sw_k, sw_v, gate_a, gate_b = _split_columns(proj, IN_SIZES)

        dn_qkv = jax.nn.silu(causal_dwconv(dn_qkv, dn_conv[l]))
        dq, dk, dv = [t.reshape(B, T, DN_HEADS, DN_HEAD_DIM) for t in jnp.split(dn_qkv, 3, axis=-1)]
        dq, dk = l2_norm(dq), l2_norm(dk)
        g = -jnp.exp(dn_a_log[l].astype(jnp.float32)) * jax.nn.softplus(
            dn_a.astype(jnp.float32) + dn_dt_bias[l].astype(jnp.float32))
        beta = jax.nn.sigmoid(dn_b.astype(jnp.float32))
        o_dn = gated_delta_rule_chunked(dq, dk, dv, g, beta)
        o_dn = rms_norm(o_dn, dn_norm[l]) * jax.nn.silu(dn_z.reshape(B, T, DN_HEADS, DN_HEAD_DIM))
        y_a = o_dn.reshape(B, T, DN_WIDTH) @ w_dn_out[l]

        sq = partial_rope(rms_norm(sw_q.reshape(B, T, SWA_Q_HEADS, SWA_HEAD_DIM), swa_q_norm[l]), positions)
        sk = partial_rope(rms_norm(sw_k.reshape(B, T, SWA_KV_HEADS, SWA_HEAD_DIM), swa_k_norm[l]), positions)
        sv = sw_v.reshape(B, T, SWA_KV_HEADS, SWA_HEAD_DIM)
        y_b = swa_sink_attention(sq, sk, sv, swa_sinks[l]) @ w_swa_out[l]

        merged = jax.nn.sigmoid(gate_a) * y_a + jax.nn.sigmoid(gate_b) * y_b
        x = x + gt1 * (merged @ w_o[l])

        h = rms_norm(x, norm_ffn[l]) * (1.0 + sc2) + sh2
        up_act, up_lin = jnp.split(h @ w_up[l], 2, axis=-1)
        up_act = causal_dwconv(up_act, ffn_conv[l]) + ffn_conv_b[l]
        x = x + gt2 * ((jax.nn.silu(up_act) * up_lin) @ w_down[l])
    return x
```

```python
import numpy as np
from contextlib import ExitStack
import concourse.bass as bass
import concourse.mybir as mybir
from concourse.bass_utils import run_bass_kernel_spmd

F32 = mybir.dt.float32
BF16 = mybir.dt.bfloat16
I32 = mybir.dt.int32
AF = mybir.ActivationFunctionType
ALU = mybir.AluOpType
AX = mybir.AxisListType

D = 1024
NT = 1024
NCH = NT // 64
NQB = NT // 128
DFF = 2816
NF = DFF // 128
EPS = 1e-6
TWO_PI = 6.283185307179586
C1 = 6.28125
C2 = TWO_PI - C1
PI = 3.141592653589793


class Dep:
    __slots__ = ("w", "r")

    def __init__(self):
        self.w = None
        self.r = []


class Sched:
    ENGS = ("pe", "dve", "act", "pool", "sp")

    def __init__(self, n_dma_slots=8):
        self.prog = {e: [] for e in self.ENGS}
        self.count = {e: 0 for e in self.ENGS}
        self.waited = {e: {} for e in self.ENGS}
        self.n_dma_slots = n_dma_slots
        self.dma_total = {}
        self.dma_next = {e: 0 for e in self.ENGS}
        self.semkeys = set(self.ENGS)
        self.n_wait = 0
        self.n_ins = 0

    def _wait(self, en, key, val):
        if val <= 0:
            return
        if en == "pe" and key == "pe":
            return
        w = self.waited[en]
        if w.get(key, 0) >= val:
            return
        w[key] = val
        self.n_wait += 1
        self.prog[en].append(("wait", key, val))

    def _deps(self, en, R, W):
        for d in R:
            if d.w is not None:
                self._wait(en, *d.w)
        for d in W:
            if d.w is not None:
                self._wait(en, *d.w)
            for r in d.r:
                self._wait(en, *r)

    def _record(self, tok, R, W):
        for d in W:
            d.w = tok
            d.r = []
        for d in R:
            d.r.append(tok)
            if len(d.r) > 16:
                best = {}
                for k, v in d.r:
                    if best.get(k, 0) < v:
                        best[k] = v
                d.r = list(best.items())

    @staticmethod
    def _flat(x):
        out = []
        for d in x:
            if isinstance(d, (list, tuple)):
                out.extend(Sched._flat(d))
            else:
                out.append(d)
        return out

    def op(self, en, fn, R=(), W=()):
        R = self._flat(R)
        W = self._flat(W)
        self._deps(en, R, W)
        self.count[en] += 1
        tok = (en, self.count[en])
        self.prog[en].append(("ins", fn, en, 1))
        self.n_ins += 1
        self._record(tok, R, W)
        return tok

    def dma(self, en, fn, R=(), W=()):
        R = self._flat(R)
        W = self._flat(W)
        self._deps(en, R, W)
        slot = self.dma_next[en]
        self.dma_next[en] = (slot + 1) % self.n_dma_slots
        key = ("dma", en, slot)
        self.semkeys.add(key)
        prev = self.dma_total.get(key, 0)
        self._wait(en, key, prev)
        tot = prev + 16
        self.dma_total[key] = tot
        self.prog[en].append(("ins", fn, key, 16))
        self.n_ins += 1
        tok = (key, tot)
        self._record(tok, R, W)
        return tok

    def finish_wait(self, en, toks):
        for t in toks:
            self._wait(en, *t)

    def build(self, nc, stack):
        sems = {}
        for k in sorted(self.semkeys, key=str):
            nm = k if isinstance(k, str) else "d_%s_%d" % (k[1], k[2])
            sems[k] = stack.enter_context(nc.semaphore("s_" + nm))
        block = stack.enter_context(nc.Block())
        prog = self.prog

        def run(eng, lst):
            for it in lst:
                if it[0] == "wait":
                    eng.wait_ge(sems[it[1]], it[2])
                else:
                    it[1](eng).then_inc(sems[it[2]], it[3])

        @block.tensor
        def _(eng):
            run(eng, prog["pe"])

        @block.vector
        def _(eng):
            run(eng, prog["dve"])

        @block.scalar
        def _(eng):
            run(eng, prog["act"])

        @block.gpsimd
        def _(eng):
            run(eng, prog["pool"])

        @block.sync
        def _(eng):
            run(eng, prog["sp"])


class Prog:
    def __init__(self, L, npass_desc, dbg=False):
        self.nc = bass.Bass("TRN2", target_bir_lowering=False)
        self.S = Sched()
        self.st = ExitStack()
        self.L = L
        self.dbg = dbg
        self.pair_rr = 0
        self.slab_rr = 0

    def sb(self, name, shape, dt):
        return self.st.enter_context(self.nc.sbuf_tensor(name, shape, dt))

    def din(self, name, shape, dt=F32):
        return self.nc.dram_tensor(name, list(shape), dt, kind="ExternalInput").ap()

    def dout(self, name, shape, dt=F32):
        return self.nc.dram_tensor(name, list(shape), dt, kind="ExternalOutput").ap()

    def mm(self, out, lhsT, rhs, start, stop, R, W):
        return self.S.op("pe", lambda e: e.matmul(out, lhsT=lhsT, rhs=rhs, start=start, stop=stop), R, W)

    def tr(self, out, in_, ident, R, W):
        return self.S.op("pe", lambda e: e.transpose(out, in_, ident), R, W)

    def act(self, out, in_, func, R, W, bias=None, scale=None, accum=None):
        kw = {}
        if bias is not None:
            kw["bias"] = bias
        if scale is not None:
            kw["scale"] = scale
        if accum is not None:
            kw["accum_out"] = accum
        return self.S.op("act", lambda e: e.activation(out=out, in_=in_, func=func, **kw), R, W)

    def cp(self, en, out, in_, R, W):
        if en == "act":
            return self.S.op("act", lambda e: e.copy(out=out, in_=in_), R, W)
        return self.S.op(en, lambda e: e.tensor_copy(out=out, in_=in_), R, W)

    def tt(self, out, in0, in1, op, R, W, en="dve"):
        return self.S.op(en, lambda e: e.tensor_tensor(out=out, in0=in0, in1=in1, op=op), R, W)

    def ts(self, out, in0, s1, s2, op0, op1, R, W, en="dve"):
        if op1 is None:
            return self.S.op(en, lambda e: e.tensor_scalar(out=out, in0=in0, scalar1=s1, scalar2=None, op0=op0), R, W)
        return self.S.op(en, lambda e: e.tensor_scalar(out=out, in0=in0, scalar1=s1, scalar2=s2, op0=op0, op1=op1), R, W)

    def stt(self, out, in0, scalar, in1, op0, op1, R, W):
        return self.S.op("dve", lambda e: e.scalar_tensor_tensor(out=out, in0=in0, scalar=scalar, in1=in1,
                                                                 op0=op0, op1=op1), R, W)

    def ld(self, out, in_, W, R=(), en="sp"):
        return self.S.dma(en, lambda e: e.dma_start(out=out, in_=in_), R, W)


def build_program(L, passes, dbg=False):
    nblk = passes
    P = Prog(L, passes, dbg)
    nc, S = P.nc, P.S
    TT = nblk * NT

    xT_d = P.din("xT", [128, 8, TT])
    cT_d = P.din("cT", [128, 8])
    pos_d = P.din("pos", [1, TT], I32)
    w_ada_d = P.din("w_ada", [L, 128, 12, 8 * 512])
    b_ada_d = P.din("b_ada", [L, 128, 48])
    nmix_d = P.din("nmix", [L, 128, 8])
    nffn_d = P.din("nffn", [L, 128, 8])
    w_in_d = P.din("w_in", [L, 128, 16, 8 * 512])
    dnconv_d = P.din("dnconv", [L, 128, 24 * 4])
    alog_d = P.din("alog", [L, 8, 1])
    dtb_d = P.din("dtb", [L, 8, 1])
    dnnorm_d = P.din("dnnorm", [L, 128, 1])
    w_dno_d = P.din("w_dno", [L, 128, 2, 8 * 512])
    qnw_d = P.din("qnw", [L, 128, 1])
    knw_d = P.din("knw", [L, 128, 1])
    sinks_d = P.din("sinks", [L, 128, 16])
    w_swo_d = P.din("w_swo", [L, 128, 2, 8 * 512])
    w_o_d = P.din("w_o", [L, 128, 2, 8 * 512])
    w_up_d = P.din("w_up", [L, 128, 11, 8 * 512])
    fconv_d = P.din("fconv", [L, 128, NF * 3])
    fconvb_d = P.din("fconvb", [L, 128, NF])
    w_dn_d = P.din("w_dn", [L, 128, 8, NF * 128])
    ident_d = P.din("ident", [128, 128])
    onesm_d = P.din("onesm", [128, 128])
    bd64_d = P.din("bd64", [128, 128])
    rm_d = P.din("rm", [128, 128])
    invf_d = P.din("invf", [1, 128])
    sel_d = P.din("sel", [8, 8 * 128])
    resetm_d = P.din("resetm", [8, NT])
    mincl_d = P.din("mincl", [128, NT])
    mstrict_d = P.din("mstrict", [128, NT])
    itile_d = P.din("itile", [128, NT])
    swmask_d = P.din("swmask", [128, 2 * 256])
    S_in_d = P.din("S_in", [L, 128, 8 * 128])
    convh_in_d = P.din("convh_in", [L, 128, 24 * 3])
    ffnh_in_d = P.din("ffnh_in", [L, 128, NF * 2])
    kh_in_d = P.din("kh_in", [L, 128, 2 * 128])
    vh_in_d = P.din("vh_in", [L, 128, 128])
    yT_d = P.dout("yT", [128, 8, TT])
    S_out_d = P.dout("S_out", [L, 128, 8 * 128])
    convh_out_d = P.dout("convh_out", [L, 128, 24 * 3])
    ffnh_out_d = P.dout("ffnh_out", [L, 128, NF * 2])
    kh_out_d = P.dout("kh_out", [L, 128, 2 * 128])
    vh_out_d = P.dout("vh_out", [L, 128, 128])
    dbg_d = P.dout("dbg", [128, 16, NT]) if dbg else None

    xT = P.sb("xT_s", [128, 8, NT], F32); d_x = [Dep() for _ in range(8)]
    hT = P.sb("hT_s", [128, 8, NT], BF16); d_h = [Dep() for _ in range(8)]
    big = P.sb("big_s", [128, 24, NT], BF16); d_big = [Dep() for _ in range(24)]
    NSLAB = 3
    slabs = [P.sb("slab%d" % i, [128, 4096], BF16) for i in range(NSLAB)]
    d_slab = [Dep() for _ in range(NSLAB)]
    NFS = 8
    FS = [P.sb("fs%d" % i, [128, NT + 8], F32) for i in range(NFS)]
    d_fs = [Dep() for _ in range(NFS)]
    NBS = 11
    BS = [P.sb("bs%d" % i, [128, NT], BF16)[:] for i in range(5)] + [big[:, 18 + i, :] for i in range(6)]
    d_bs = [Dep() for _ in range(5)] + [d_big[18 + i] for i in range(6)]
    TOK = [big[:, 8 + 2 * i:10 + 2 * i, :].rearrange("p a b -> p (a b)") for i in range(3)]
    d_tok = [[d_big[8 + 2 * i], d_big[9 + 2 * i]] for i in range(3)]
    u_sb = big[:, 14:18, :].bitcast(F32).rearrange("p a b -> p (a b)")
    d_u = [d_big[14 + i] for i in range(4)]
    ident = P.sb("ident_s", [128, 128], F32); d_ident = Dep()
    identb = P.sb("identb_s", [128, 128], BF16); d_identb = Dep()
    onesm = P.sb("onesm_s", [128, 128], F32); d_onesm = Dep()
    bd64 = P.sb("bd64_s", [128, 128], F32); d_bd64 = Dep()
    rm = P.sb("rm_s", [128, 128], F32); d_rm = Dep()
    invf = P.sb("invf_s", [1, 128], F32); d_invf = Dep()
    sel = P.sb("sel_s", [8, 8 * 128], F32); d_sel = Dep()
    mincl = P.sb("mincl_s", [128, NT], BF16); d_mincl = Dep()
    mstrict = P.sb("mstrict_s", [128, NT], BF16); d_mstrict = Dep()
    itile = P.sb("itile_s", [128, NT], BF16); d_itile = Dep()
    swmask = P.sb("swmask_s", [128, 2 * 256], F32); d_swmask = Dep()
    cst = P.sb("cst_s", [128, 8], F32); d_cst = Dep()
    cT = P.sb("cT_s", [128, 8], F32); d_cT = Dep()
    cact = P.sb("cact_s", [128, 8], BF16); d_cact = Dep()
    mod = P.sb("mod_s", [128, 48], F32); d_mod = Dep()
    b_ada = P.sb("b_ada_s", [128, 48], F32); d_bada = Dep()
    nmix = P.sb("nmix_s", [128, 8], F32); d_nmix = Dep()
    nffn = P.sb("nffn_s", [128, 8], F32); d_nffn = Dep()
    a1 = P.sb("a1_s", [128, 16], F32); d_a1 = Dep()
    dnconv = P.sb("dnconv_s", [128, 24 * 4], F32); d_dnconv = Dep()
    alog = P.sb("alog_s", [8, 1], F32); d_alog = Dep()
    nega = P.sb("nega_s", [8, 1], F32); d_nega = Dep()
    dtb = P.sb("dtb_s", [8, 1], F32); d_dtb = Dep()
    dnnorm = P.sb("dnnorm_s", [128, 1], F32); d_dnnorm = Dep()
    qnw = P.sb("qnw_s", [128, 1], F32); d_qnw = Dep()
    knw = P.sb("knw_s", [128, 1], F32); d_knw = Dep()
    sinks = P.sb("sinks_s", [128, 16], F32); d_sinks = Dep()
    fconv = P.sb("fconv_s", [128, NF * 3], F32); d_fconv = Dep()
    fconvb = P.sb("fconvb_s", [128, NF], F32); d_fconvb = Dep()
    g_gc = P.sb("g_gc", [8, NT], F32); d_gc = Dep()
    g_beta = P.sb("g_beta", [8, NT], F32); d_beta = Dep()
    egl = P.sb("egl_s", [8, 2 * NCH], F32); d_egl = Dep()
    eglb = P.sb("eglb_s", [128, 8 * NCH], F32); d_eglb = Dep()
    glb = P.sb("glb_s", [128, 8 * NCH], F32); d_glb = Dep()
    gcT = P.sb("gcT_s", [64, NCH * 8], F32); d_gcT = Dep()
    Sst = [P.sb("S_st%d" % l, [128, 8 * 128], F32) for l in range(L)]; d_Sst = [[Dep() for _ in range(8)] for _ in range(L)]
    Sbf = P.sb("S_bf", [128, 128], BF16); d_Sbf = Dep()
    convh = [P.sb("convh%d" % l, [128, 24 * 3], F32) for l in range(L)]; d_convh = [Dep() for _ in range(L)]
    ffnh = [P.sb("ffnh%d" % l, [128, NF * 2], F32) for l in range(L)]; d_ffnh = [Dep() for _ in range(L)]
    ksw = P.sb("ksw_s", [128, 2, 128 + NT], BF16); d_ksw = [Dep(), Dep()]
    vsw = P.sb("vsw_s", [128, 1 + NQB, 128], BF16); d_vsw = Dep()
    khf = [P.sb("khf%d" % l, [128, 2 * 128], F32) for l in range(L)]; d_khf = [Dep() for _ in range(L)]
    vhf = [P.sb("vhf%d" % l, [128, 128], F32) for l in range(L)]; d_vhf = [Dep() for _ in range(L)]
    sm = P.sb("sm_s", [128, 16], F32); d_sm = Dep()
    PSB = [P.st.enter_context(nc.psum_tensor("psb%d" % i, [128, 1024], F32)) for i in range(4)]
    d_ps = [Dep() for _ in range(8)]

    def nextpair():
        k = P.pair_rr
        P.pair_rr = (k + 1) % 4
        return k

    def pb(k, half):
        return PSB[k][:, half * 512:(half + 1) * 512]

    def load_slab(src_ap, ncols):
        i = P.slab_rr
        P.slab_rr = (i + 1) % NSLAB
        S.dma("pool", lambda e: e.dma_start(out=slabs[i][:, 0:ncols], in_=src_ap), W=[d_slab[i]])
        return i

    def slab_view(i, nk, cw):
        return slabs[i][:, 0:nk * cw].rearrange("p (k c) -> p k c", k=nk)

    c_one = cst[:, 0:1]
    c_eps = cst[:, 1:2]
    c_zero = cst[:, 2:3]

    S.op("dve", lambda e: e.memset(cst[:, 0:1], 1.0), W=[d_cst])
    S.op("dve", lambda e: e.memset(cst[:, 1:2], EPS), W=[d_cst])
    S.op("dve", lambda e: e.memset(cst[:, 2:3], 0.0), W=[d_cst])
    P.ld(ident[:], ident_d, [d_ident])
    P.ld(onesm[:], onesm_d, [d_onesm])
    P.ld(bd64[:], bd64_d, [d_bd64])
    P.ld(rm[:], rm_d, [d_rm])
    P.ld(invf[:], invf_d, [d_invf])
    P.ld(sel[:], sel_d, [d_sel])
    P.ld(swmask[:], swmask_d, [d_swmask])
    P.ld(cT[:], cT_d, [d_cT])
    for (dst, dd, src) in ((mincl, d_mincl, mincl_d), (mstrict, d_mstrict, mstrict_d), (itile, d_itile, itile_d)):
        P.ld(FS[0][:, 0:NT], src, [d_fs[0]])
        P.cp("dve", dst[:], FS[0][:, 0:NT], [d_fs[0]], [dd])
    P.cp("dve", identb[:], ident[:], [d_ident], [d_identb])
    P.act(cact[:], cT[:], AF.Silu, [d_cT], [d_cact])
    for l in range(L):
        P.ld(Sst[l][:], S_in_d[l], d_Sst[l])
        P.ld(convh[l][:], convh_in_d[l], [d_convh[l]])
        P.ld(ffnh[l][:], ffnh_in_d[l], [d_ffnh[l]])
        P.ld(khf[l][:], kh_in_d[l], [d_khf[l]])
        P.ld(vhf[l][:], vh_in_d[l], [d_vhf[l]])

    dbg_slot = [0]

    def dbg_out(ap_f32_1024, R):
        if not dbg or dbg_slot[0] >= 16:
            return
        i = dbg_slot[0]
        dbg_slot[0] += 1
        S.dma("sp", lambda e: e.dma_start(out=dbg_d[:, i, :], in_=ap_f32_1024), R=R)

    out_toks = []

    def emit_pass(li, kb):
        t0 = kb * NT
        if kb == 0:
            P.ld(b_ada[:], b_ada_d[li], [d_bada])
            P.ld(nmix[:], nmix_d[li], [d_nmix])
            P.ld(nffn[:], nffn_d[li], [d_nffn])
            P.ld(dnconv[:], dnconv_d[li], [d_dnconv])
            P.ld(alog[:], alog_d[li], [d_alog])
            P.ld(dtb[:], dtb_d[li], [d_dtb])
            P.ld(dnnorm[:], dnnorm_d[li], [d_dnnorm])
            P.ld(qnw[:], qnw_d[li], [d_qnw])
            P.ld(knw[:], knw_d[li], [d_knw])
            P.ld(sinks[:], sinks_d[li], [d_sinks])
            P.ld(fconv[:], fconv_d[li], [d_fconv])
            P.ld(fconvb[:], fconvb_d[li], [d_fconvb])
            k = nextpair()
            for sidx in range(12):
                si = load_slab(w_ada_d[li, :, sidx], 4096)
                wv = slab_view(si, 8, 512)
                for jj in range(4):
                    j = sidx * 4 + jj
                    for kc in range(8):
                        P.mm(PSB[k][:, j:j + 1], wv[:, kc, jj * 128:(jj + 1) * 128], cact[:, kc:kc + 1],
                             kc == 0, kc == 7, [d_slab[si], d_cact], [d_ps[2 * k]])
            P.tt(mod[:], PSB[k][:, 0:48], b_ada[:], ALU.add, [d_ps[2 * k], d_bada], [d_mod])
            P.stt(a1[:, 0:8], mod[:, 8:16], 1.0, nmix[:], ALU.add, ALU.mult, [d_mod, d_nmix], [d_a1])
            P.stt(a1[:, 8:16], mod[:, 32:40], 1.0, nffn[:], ALU.add, ALU.mult, [d_mod, d_nffn], [d_a1])
            P.act(nega[:], alog[:], AF.Exp, [d_alog], [d_nega])
            P.ts(nega[:], nega[:], -1.0, None, ALU.mult, None, [d_nega], [d_nega])
        src = xT_d if li == 0 else yT_d
        for c in range(8):
            Rr = []
            P.ld(xT[:, c, :], src[:, c, t0:t0 + NT], [d_x[c]])

        def norm_mod(acol, shcol):
            k = nextpair()
            for c in range(8):
                f = FS[c % 2]
                dfx = d_fs[c % 2]
                P.act(f[:, 0:NT], xT[:, c, :], AF.Square, [d_x[c]], [dfx])
                for hf in range(2):
                    P.mm(pb(k, hf), onesm[:], f[:, hf * 512:(hf + 1) * 512], c == 0, c == 7,
                         [d_onesm, dfx], [d_ps[2 * k + hf]])
            rs = FS[2]
            for hf in range(2):
                P.act(rs[:, hf * 512:(hf + 1) * 512], pb(k, hf), AF.Sqrt, [d_ps[2 * k + hf], d_cst], [d_fs[2]],
                      bias=c_eps, scale=1.0 / D)
            S.op("dve", lambda e: e.reciprocal(out=rs[:, 0:NT], in_=rs[:, 0:NT]), [d_fs[2]], [d_fs[2]])
            for c in range(8):
                f = FS[3 + c % 2]
                dfx = d_fs[3 + c % 2]
                P.tt(f[:, 0:NT], xT[:, c, :], rs[:, 0:NT], ALU.mult, [d_x[c], d_fs[2]], [dfx])
                P.act(hT[:, c, :], f[:, 0:NT], AF.Identity, [dfx, d_a1, d_mod], [d_h[c]],
                      bias=mod[:, shcol + c:shcol + c + 1], scale=a1[:, acol + c:acol + c + 1])

        norm_mod(0, 0)

        def proj_fm(si, kcn, col0, rhs_t, d_rhs, tile0=0):
            k = nextpair()
            wv = slab_view(si, kcn, slabw[0])
            for hf in range(2):
                for kc in range(kcn):
                    P.mm(pb(k, hf), wv[:, kc, col0:col0 + 128],
                         rhs_t[:, tile0 + kc, hf * 512:(hf + 1) * 512],
                         kc == 0, kc == kcn - 1, [d_slab[si], d_rhs[tile0 + kc]], [d_ps[2 * k + hf]])
            return k

        slabw = [512]

        si_ab = load_slab(w_in_d[li, :, 8], 4096)
        wv = slab_view(si_ab, 8, 512)
        ka = nextpair()
        kbp = nextpair()
        for (kk, c0) in ((ka, 0), (kbp, 8)):
            for hf in range(2):
                for kc in range(8):
                    P.mm(PSB[kk][0:8, hf * 512:(hf + 1) * 512], wv[:, kc, c0:c0 + 8],
                         hT[:, kc, hf * 512:(hf + 1) * 512], kc == 0, kc == 7,
                         [d_slab[si_ab], d_h[kc]], [d_ps[2 * kk + hf]])
        g_tmp, g_g, g_rm = FS[0][0:8, 0:NT], FS[1][0:8, 0:NT], FS[2][0:8, 0:NT]
        P.ld(g_rm, resetm_d, [d_fs[2]])
        for hf in range(2):
            sl = slice(hf * 512, (hf + 1) * 512)
            P.act(g_tmp[:, sl], PSB[ka][0:8, sl], AF.Exp, [d_ps[2 * ka + hf], d_dtb], [d_fs[0]], bias=dtb[:, 0:1], scale=1.0)
            P.act(g_beta[:, sl], PSB[kbp][0:8, sl], AF.Sigmoid, [d_ps[2 * kbp + hf]], [d_beta])
        P.act(g_tmp, g_tmp, AF.Ln, [d_fs[0], d_cst], [d_fs[0]], bias=cst[0:8, 0:1], scale=1.0)
        P.ts(g_g, g_tmp, nega[:, 0:1], None, ALU.mult, None, [d_fs[0], d_nega], [d_fs[1]])
        S.op("dve", lambda e: e.tensor_tensor_scan(out=g_gc[:], data0=g_rm, data1=g_g, initial=0.0,
                                                   op0=ALU.mult, op1=ALU.add), [d_fs[2], d_fs[1]], [d_gc])
        gc3 = g_gc[:].rearrange("p (n c) -> p n c", c=64)
        P.act(egl[:, 0:NCH].rearrange("p (n c) -> p n c", c=1), gc3[:, :, 63:64], AF.Exp, [d_gc], [d_egl])
        P.cp("dve", egl[:, NCH:2 * NCH].rearrange("p (n c) -> p n c", c=1), gc3[:, :, 63:64], [d_gc], [d_egl])
        k = nextpair()
        for h in range(8):
            P.mm(PSB[k][:, h * NCH:(h + 1) * NCH], sel[:, h * 128:(h + 1) * 128], egl[:, 0:NCH], True, True,
                 [d_sel, d_egl], [d_ps[2 * k]])
        for h in range(8):
            P.mm(PSB[k][:, 512 + h * NCH:512 + (h + 1) * NCH], sel[:, h * 128:(h + 1) * 128], egl[:, NCH:2 * NCH], True, True,
                 [d_sel, d_egl], [d_ps[2 * k + 1]])
        P.cp("act", eglb[:], PSB[k][:, 0:8 * NCH], [d_ps[2 * k]], [d_eglb])
        P.cp("act", glb[:], PSB[k][:, 512:512 + 8 * NCH], [d_ps[2 * k + 1]], [d_glb])
        k = nextpair()
        for n in range(NCH):
            P.tr(PSB[k][0:64, n * 8:(n + 1) * 8], g_gc[0:8, n * 64:(n + 1) * 64], ident[0:8, 0:8],
                 [d_gc, d_ident], [d_ps[2 * k]])
        P.cp("act", gcT[:], PSB[k][0:64, 0:NCH * 8], [d_ps[2 * k]], [d_gcT])

        f_P, f_q, f_k, f_v, f_sq, f_rs, f_z, f_oT = FS[0], FS[1], FS[2], FS[3], FS[4], FS[5], FS[6], FS[7]
        iP, iq, ik, iv, isq, irs, iz, ioT = range(8)
        f_x, ix = f_oT, ioT
        (b_kT, b_kb, b_bv, b_kbe, b_qd, b_kd, b_qT, b_qk, b_Tt, b_wT, b_x) = BS
        for h in range(8):
            si = load_slab(w_in_d[li, :, h], 4096)
            for which, (fdst, idst) in enumerate(((f_q, iq), (f_k, ik), (f_v, iv))):
                ci = which * 8 + h
                k = proj_fm(si, 8, which * 128, hT, d_h)
                P.cp("pool", f_P[:, 0:3], convh[li][:, ci * 3:ci * 3 + 3], [d_convh[li]], [d_fs[iP]])
                for hf in range(2):
                    P.cp("act", f_P[:, 3 + hf * 512:3 + (hf + 1) * 512], pb(k, hf), [d_ps[2 * k + hf]], [d_fs[iP]])
                P.cp("pool", convh[li][:, ci * 3:ci * 3 + 3], f_P[:, NT:NT + 3], [d_fs[iP]], [d_convh[li]])
                wc = lambda j: dnconv[:, ci * 4 + j:ci * 4 + j + 1]
                P.ts(fdst[:, 0:NT], f_P[:, 3:3 + NT], wc(3), None, ALU.mult, None, [d_fs[iP], d_dnconv], [d_fs[idst]])
                for j in (2, 1, 0):
                    P.stt(fdst[:, 0:NT], f_P[:, j:j + NT], wc(j), fdst[:, 0:NT], ALU.mult, ALU.add,
                          [d_fs[iP], d_dnconv, d_fs[idst]], [d_fs[idst]])
                P.act(fdst[:, 0:NT], fdst[:, 0:NT], AF.Silu, [d_fs[idst]], [d_fs[idst]])
            k = proj_fm(si, 8, 384, hT, d_h)
            for hf in range(2):
                P.act(f_z[:, hf * 512:(hf + 1) * 512], pb(k, hf), AF.Silu, [d_ps[2 * k + hf]], [d_fs[iz]])
            for (fx, ifx, scl) in ((f_q, iq, 128.0 ** -0.5), (f_k, ik, 1.0)):
                P.act(f_sq[:, 0:NT], fx[:, 0:NT], AF.Square, [d_fs[ifx]], [d_fs[isq]])
                k = nextpair()
                for hf in range(2):
                    P.mm(pb(k, hf), onesm[:], f_sq[:, hf * 512:(hf + 1) * 512], True, True, [d_onesm, d_fs[isq]],
                         [d_ps[2 * k + hf]])
                    P.act(f_rs[:, hf * 512:(hf + 1) * 512], pb(k, hf), AF.Sqrt, [d_ps[2 * k + hf], d_cst], [d_fs[irs]],
                          bias=c_eps, scale=1.0)
                S.op("dve", lambda e: e.reciprocal(out=f_rs[:, 0:NT], in_=f_rs[:, 0:NT]), [d_fs[irs]], [d_fs[irs]])
                P.stt(fx[:, 0:NT], fx[:, 0:NT], scl, f_rs[:, 0:NT], ALU.mult, ALU.mult, [d_fs[ifx], d_fs[irs]], [d_fs[ifx]])
            if dbg and h == 0 and li == 0 and kb == 0:
                dbg_out(f_q[:, 0:NT], [d_fs[iq]])
                dbg_out(f_k[:, 0:NT], [d_fs[ik]])
                dbg_out(f_v[:, 0:NT], [d_fs[iv]])
            P.cp("pool", b_kT[:], f_k[:, 0:NT], [d_fs[ik]], [d_bs[0]])
            P.cp("pool", b_qT[:], f_q[:, 0:NT], [d_fs[iq]], [d_bs[6]])

            def bcast(grow, dg):
                k = nextpair()
                for hf in range(2):
                    P.mm(pb(k, hf), sel[:, h * 128:(h + 1) * 128], grow[:, hf * 512:(hf + 1) * 512], True, True,
                         [d_sel, dg], [d_ps[2 * k + hf]])
                return k

            kbt = bcast(g_beta, d_beta)
            kgc = bcast(g_gc, d_gc)
            f_E, iE, f_D, iD, f_BE, iBE = f_P, iP, f_oT, ioT, f_rs, irs
            glb3 = glb[:, h * NCH:(h + 1) * NCH].rearrange("p (n c) -> p n c", c=1)
            for hf in range(2):
                sl = slice(hf * 512, (hf + 1) * 512)
                P.act(f_E[:, sl], pb(kgc, hf), AF.Exp, [d_ps[2 * kgc + hf]], [d_fs[iE]])
                P.tt(f_D[:, sl].rearrange("p (n c) -> p n c", c=64),
                     glb3[:, hf * 8:(hf + 1) * 8, :].broadcast_to([128, 8, 64]),
                     pb(kgc, hf).rearrange("p (n c) -> p n c", c=64), ALU.subtract,
                     [d_glb, d_ps[2 * kgc + hf]], [d_fs[iD]])
                P.act(f_D[:, sl], f_D[:, sl], AF.Exp, [d_fs[iD]], [d_fs[iD]])
                P.tt(b_kb[:, sl], f_k[:, sl], pb(kbt, hf), ALU.mult, [d_fs[ik], d_ps[2 * kbt + hf]], [d_bs[1]])
                P.tt(b_bv[:, sl], f_v[:, sl], pb(kbt, hf), ALU.mult, [d_fs[iv], d_ps[2 * kbt + hf]], [d_bs[2]])
                P.tt(f_BE[:, sl], f_E[:, sl], pb(kbt, hf), ALU.mult, [d_fs[iE], d_ps[2 * kbt + hf]], [d_fs[iBE]])
                P.tt(b_kbe[:, sl], f_k[:, sl], f_BE[:, sl], ALU.mult, [d_fs[ik], d_fs[iBE]], [d_bs[3]])
                P.tt(b_qd[:, sl], f_q[:, sl], f_E[:, sl], ALU.mult, [d_fs[iq], d_fs[iE]], [d_bs[4]])
                P.tt(b_kd[:, sl], f_k[:, sl], f_D[:, sl], ALU.mult, [d_fs[ik], d_fs[iD]], [d_bs[5]])
            k = kgc
            for n in range(NCH):
                hf = n // 8
                P.ts(f_sq[0:64, n * 64:(n + 1) * 64], PSB[k][0:64, n * 64:(n + 1) * 64], gcT[:, n * 8 + h:n * 8 + h + 1], 0.0,
                     ALU.subtract, ALU.min, [d_ps[2 * k + hf], d_gcT], [d_fs[isq]])
            P.act(f_sq[0:64, 0:NT], f_sq[0:64, 0:NT], AF.Exp, [d_fs[isq]], [d_fs[isq]])
            P.tt(f_rs[0:64, 0:NT], f_sq[0:64, 0:NT], mstrict[0:64, :], ALU.mult, [d_fs[isq], d_mstrict], [d_fs[irs]])
            P.tt(f_sq[0:64, 0:NT], f_sq[0:64, 0:NT], mincl[0:64, :], ALU.mult, [d_fs[isq], d_mincl], [d_fs[isq]])
            for (bsrc, ib, tdst, it) in ((b_kbe, 3, TOK[0], 0), (b_bv, 2, TOK[1], 1), (b_kd, 5, TOK[2], 2)):
                for half in range(2):
                    k = nextpair()
                    pv = PSB[k][:, 0:512].bitcast(BF16)
                    for n8 in range(8):
                        n = half * 8 + n8
                        P.tr(pv[0:64, n8 * 128:(n8 + 1) * 128], bsrc[:, n * 64:(n + 1) * 64], identb[:],
                             [d_bs[ib], d_identb], [d_ps[2 * k]])
                    P.cp("act", tdst[0:64, half * NT:(half + 1) * NT], pv[0:64, :], [d_ps[2 * k]], [d_tok[it]])
            kK = nextpair()
            for n in range(NCH):
                hf = n // 8
                P.mm(PSB[kK][0:64, n * 64:(n + 1) * 64], b_kT[:, n * 64:(n + 1) * 64], b_kb[:, n * 64:(n + 1) * 64],
                     True, True, [d_bs[0], d_bs[1]], [d_ps[2 * kK + hf]])
            kQ = nextpair()
            for n in range(NCH):
                hf = n // 8
                P.mm(PSB[kQ][0:64, n * 64:(n + 1) * 64], b_kT[:, n * 64:(n + 1) * 64], b_qT[:, n * 64:(n + 1) * 64],
                     True, True, [d_bs[0], d_bs[6]], [d_ps[2 * kQ + hf]])
            f_xa, ixa = f_x, ix
            for hf in range(2):
                sl = slice(hf * 512, (hf + 1) * 512)
                P.stt(f_xa[0:64, sl], PSB[kK][0:64, sl], -1.0, f_rs[0:64, sl], ALU.mult, ALU.mult,
                      [d_ps[2 * kK + hf], d_fs[irs]], [d_fs[ixa]])
                P.tt(b_qk[0:64, sl], PSB[kQ][0:64, sl], f_sq[0:64, sl], ALU.mult, [d_ps[2 * kQ + hf], d_fs[isq]], [d_bs[7]])
            Xb = [(f_x, ix), (f_q, iq)]
            XTb = [(f_k, ik), (f_v, iv)]
            fPp, iPp = f_P, iP

            def transpose_X(srcf, isrc, dstf, idst):
                k = nextpair()
                for n in range(NCH):
                    hf = n // 8
                    P.tr(PSB[k][0:64, n * 64:(n + 1) * 64], srcf[0:64, n * 64:(n + 1) * 64], ident[0:64, 0:64],
                         [d_fs[isrc], d_ident], [d_ps[2 * k + hf]])
                for hf in range(2):
                    sl = slice(hf * 512, (hf + 1) * 512)
                    P.cp("act", dstf[0:64, sl], PSB[k][0:64, sl], [d_ps[2 * k + hf]], [d_fs[idst]])

            transpose_X(Xb[0][0], Xb[0][1], XTb[0][0], XTb[0][1])
            P.tt(fPp[0:64, 0:NT], Xb[0][0][0:64, 0:NT], itile[0:64, :], ALU.add, [d_fs[Xb[0][1]], d_itile], [d_fs[iPp]])
            cur = 0
            for step in range(5):
                (Xc, iXc), (XTc, iXTc) = Xb[cur], XTb[cur]
                (Xn, iXn), (XTn, iXTn) = Xb[1 - cur], XTb[1 - cur]
                k1 = nextpair()
                for n in range(NCH):
                    hf = n // 8
                    cs = slice(n * 64, (n + 1) * 64)
                    P.mm(PSB[k1][0:64, cs], XTc[0:64, cs], Xc[0:64, cs], True, True, [d_fs[iXTc], d_fs[iXc]],
                         [d_ps[2 * k1 + hf]])
                k2 = nextpair()
                for n in range(NCH):
                    hf = n // 8
                    cs = slice(n * 64, (n + 1) * 64)
                    P.mm(PSB[k2][0:64, cs], Xc[0:64, cs], XTc[0:64, cs], True, True, [d_fs[iXTc], d_fs[iXc]],
                         [d_ps[2 * k2 + hf]])
                for hf in range(2):
                    sl = slice(hf * 512, (hf + 1) * 512)
                    P.cp("act", Xn[0:64, sl], PSB[k1][0:64, sl], [d_ps[2 * k1 + hf]], [d_fs[iXn]])
                    P.cp("act", XTn[0:64, sl], PSB[k2][0:64, sl], [d_ps[2 * k2 + hf]], [d_fs[iXTn]])
                k3 = nextpair()
                for n in range(NCH):
                    hf = n // 8
                    cs = slice(n * 64, (n + 1) * 64)
                    P.mm(PSB[k3][0:64, cs], XTn[0:64, cs], fPp[0:64, cs], True, True, [d_fs[iXTn], d_fs[iPp]],
                         [d_ps[2 * k3 + hf]])
                for hf in range(2):
                    sl = slice(hf * 512, (hf + 1) * 512)
                    P.tt(fPp[0:64, sl], fPp[0:64, sl], PSB[k3][0:64, sl], ALU.add, [d_fs[iPp], d_ps[2 * k3 + hf]], [d_fs[iPp]])
                cur = 1 - cur
            P.cp("dve", b_Tt[0:64, :], fPp[0:64, 0:NT], [d_fs[iPp]], [d_bs[8]])
            for q4 in range(4):
                k = nextpair()
                for n4 in range(4):
                    n = q4 * 4 + n4
                    P.mm(PSB[k][0:64, n4 * 128:(n4 + 1) * 128], b_Tt[0:64, n * 64:(n + 1) * 64],
                         TOK[1][0:64, n * 128:(n + 1) * 128], True, True, [d_bs[8], d_tok[1]], [d_ps[2 * k]])
                P.cp("act", u_sb[0:64, q4 * 512:(q4 + 1) * 512], PSB[k][0:64, 0:512], [d_ps[2 * k]], [d_u])
            k = nextpair()
            for n in range(NCH):
                hf = n // 8
                P.mm(PSB[k][:, n * 64:(n + 1) * 64], TOK[0][0:64, n * 128:(n + 1) * 128], b_Tt[0:64, n * 64:(n + 1) * 64],
                     True, True, [d_bs[8], d_tok[0]], [d_ps[2 * k + hf]])
            for hf in range(2):
                sl = slice(hf * 512, (hf + 1) * 512)
                P.cp("act", b_wT[:, sl], PSB[k][:, sl], [d_ps[2 * k + hf]], [d_bs[9]])
            Sh = Sst[li][:, h * 128:(h + 1) * 128]
            dS = d_Sst[li][h]
            P.cp("act", Sbf[:], Sh, [dS], [d_Sbf])
            ko = nextpair()
            kw_ = nextpair()
            ks_ = nextpair()
            for n in range(NCH):
                hf = n // 8
                cs = slice(n * 64, (n + 1) * 64)
                par = n % 2
                ws_ps = PSB[kw_][0:64, par * 512:par * 512 + 128]
                P.mm(ws_ps, b_wT[:, cs], Sbf[:], True, True, [d_bs[9], d_Sbf], [d_ps[2 * kw_ + par]])
                vnew = b_x[0:64, par * 128:(par + 1) * 128]
                dvn = d_bs[10]
                P.tt(vnew, u_sb[0:64, n * 128:(n + 1) * 128], ws_ps, ALU.subtract, [d_u, d_ps[2 * kw_ + par]], [dvn])
                P.mm(PSB[ko][:, cs], Sbf[:], b_qd[:, cs], True, False, [d_Sbf, d_bs[4]], [d_ps[2 * ko + hf]])
                P.mm(PSB[ko][:, cs], vnew, b_qk[0:64, cs], False, True, [dvn, d_bs[7]], [d_ps[2 * ko + hf]])
                S_ps = PSB[ks_][:, par * 512:par * 512 + 128]
                P.mm(S_ps, TOK[2][0:64, n * 128:(n + 1) * 128], vnew, True, True, [d_tok[2], dvn], [d_ps[2 * ks_ + par]])
                P.stt(Sh, Sh, eglb[:, h * NCH + n:h * NCH + n + 1], S_ps, ALU.mult, ALU.add,
                      [dS, d_eglb, d_ps[2 * ks_ + par]], [dS])
                P.cp("act", Sbf[:], Sh, [dS], [d_Sbf])
            for hf in range(2):
                sl = slice(hf * 512, (hf + 1) * 512)
                P.cp("act", f_oT[:, sl], PSB[ko][:, sl], [d_ps[2 * ko + hf]], [d_fs[ioT]])
            P.act(f_sq[:, 0:NT], f_oT[:, 0:NT], AF.Square, [d_fs[ioT]], [d_fs[isq]])
            k = nextpair()
            for hf in range(2):
                P.mm(pb(k, hf), onesm[:], f_sq[:, hf * 512:(hf + 1) * 512], True, True, [d_onesm, d_fs[isq]],
                     [d_ps[2 * k + hf]])
                P.act(f_rs[:, hf * 512:(hf + 1) * 512], pb(k, hf), AF.Sqrt, [d_ps[2 * k + hf], d_cst], [d_fs[irs]],
                      bias=c_eps, scale=1.0 / 128.0)
            S.op("dve", lambda e: e.reciprocal(out=f_rs[:, 0:NT], in_=f_rs[:, 0:NT]), [d_fs[irs]], [d_fs[irs]])
            P.stt(f_oT[:, 0:NT], f_oT[:, 0:NT], dnnorm[:, 0:1], f_rs[:, 0:NT], ALU.mult, ALU.mult,
                  [d_fs[ioT], d_dnnorm, d_fs[irs]], [d_fs[ioT]])
            P.tt(big[:, h, :], f_oT[:, 0:NT], f_z[:, 0:NT], ALU.mult, [d_fs[ioT], d_fs[iz]], [d_big[h]])
            if dbg and h == 0 and li == 0 and kb == 0:
                P.tt(f_oT[:, 0:NT], f_oT[:, 0:NT], f_z[:, 0:NT], ALU.mult, [d_fs[ioT], d_fs[iz]], [d_fs[ioT]])
                dbg_out(f_oT[:, 0:NT], [d_fs[ioT]])

        f_cos, icos, f_sin, isin = FS[7], 7, FS[6], 6
        f_a, ia, f_b, ib_ = FS[4], 4, FS[5], 5
        posi = FS[1][0:1, 0:NT].bitcast(I32)
        posf = FS[2][0:1, 0:NT]
        d_posi, d_posf = d_fs[1], d_fs[2]
        P.ld(posi, pos_d[:, t0:t0 + NT], [d_posi])
        P.cp("dve", posf, posi, [d_posi], [d_posf])
        k = nextpair()
        for hf in range(2):
            P.mm(pb(k, hf), invf[:], posf[:, hf * 512:(hf + 1) * 512], True, True, [d_invf, d_posf], [d_ps[2 * k + hf]])
            P.cp("act", f_a[:, hf * 512:(hf + 1) * 512], pb(k, hf), [d_ps[2 * k + hf]], [d_fs[ia]])
        fi = FS[0]
        P.ts(f_b[:, 0:NT], f_a[:, 0:NT], 1.0 / TWO_PI, None, ALU.mult, None, [d_fs[ia]], [d_fs[ib_]])
        fi_i = fi[:, 0:NT].bitcast(I32)
        P.cp("dve", fi_i, f_b[:, 0:NT], [d_fs[ib_]], [d_fs[0]])
        P.cp("dve", f_b[:, 0:NT], fi_i, [d_fs[0]], [d_fs[ib_]])
        P.stt(f_a[:, 0:NT], f_b[:, 0:NT], -C1, f_a[:, 0:NT], ALU.mult, ALU.add, [d_fs[ib_], d_fs[ia]], [d_fs[ia]])
        P.stt(f_a[:, 0:NT], f_b[:, 0:NT], -C2, f_a[:, 0:NT], ALU.mult, ALU.add, [d_fs[ib_], d_fs[ia]], [d_fs[ia]])

        def wrap_sin(dst, idst, shift):
            P.ts(f_b[:, 0:NT], f_a[:, 0:NT], shift, None, ALU.add, None, [d_fs[ia]], [d_fs[ib_]])
            P.ts(fi[:, 0:NT], f_b[:, 0:NT], PI, -TWO_PI, ALU.is_gt, ALU.mult, [d_fs[ib_]], [d_fs[0]])
            P.tt(f_b[:, 0:NT], f_b[:, 0:NT], fi[:, 0:NT], ALU.add, [d_fs[ib_], d_fs[0]], [d_fs[ib_]])
            P.ts(fi[:, 0:NT], f_b[:, 0:NT], -PI, TWO_PI, ALU.is_lt, ALU.mult, [d_fs[ib_]], [d_fs[0]])
            P.tt(f_b[:, 0:NT], f_b[:, 0:NT], fi[:, 0:NT], ALU.add, [d_fs[ib_], d_fs[0]], [d_fs[ib_]])
            P.ts(f_b[:, 0:NT], f_b[:, 0:NT], PI, -PI, ALU.min, ALU.max, [d_fs[ib_]], [d_fs[ib_]])
            P.act(dst[:, 0:NT], f_b[:, 0:NT], AF.Sin, [d_fs[ib_]], [d_fs[idst]])

        wrap_sin(f_sin, isin, 0.0)
        wrap_sin(f_cos, icos, PI / 2)

        qsw = big[:, 16:24, :]

        def qk_norm_rope(k, wcol, d_w, dst_ap, d_dst):
            f_n, in_ = FS[0], 0
            f_s, is_ = FS[1], 1
            f_r, ir = FS[2], 2
            f_t, it = FS[3], 3
            for hf in range(2):
                sl = slice(hf * 512, (hf + 1) * 512)
                P.act(f_s[:, sl], pb(k, hf), AF.Square, [d_ps[2 * k + hf]], [d_fs[is_]])
            k2 = nextpair()
            for hf in range(2):
                sl = slice(hf * 512, (hf + 1) * 512)
                P.mm(pb(k2, hf), bd64[:], f_s[:, sl], True, True, [d_bd64, d_fs[is_]], [d_ps[2 * k2 + hf]])
                P.act(f_r[:, sl], pb(k2, hf), AF.Sqrt, [d_ps[2 * k2 + hf], d_cst], [d_fs[ir]], bias=c_eps, scale=1.0 / 64.0)
            S.op("dve", lambda e: e.reciprocal(out=f_r[:, 0:NT], in_=f_r[:, 0:NT]), [d_fs[ir]], [d_fs[ir]])
            for hf in range(2):
                sl = slice(hf * 512, (hf + 1) * 512)
                P.stt(f_n[:, sl], pb(k, hf), wcol, f_r[:, sl], ALU.mult, ALU.mult, [d_ps[2 * k + hf], d_w, d_fs[ir]], [d_fs[in_]])
            k3 = nextpair()
            for hf in range(2):
                sl = slice(hf * 512, (hf + 1) * 512)
                P.mm(pb(k3, hf), rm[:], f_n[:, sl], True, True, [d_rm, d_fs[in_]], [d_ps[2 * k3 + hf]])
                P.tt(f_t[:, sl], pb(k3, hf), f_sin[:, sl], ALU.mult, [d_ps[2 * k3 + hf], d_fs[isin]], [d_fs[it]])
            P.tt(f_n[:, 0:NT], f_n[:, 0:NT], f_cos[:, 0:NT], ALU.mult, [d_fs[in_], d_fs[icos]], [d_fs[in_]])
            P.tt(dst_ap, f_n[:, 0:NT], f_t[:, 0:NT], ALU.add, [d_fs[in_], d_fs[it]], d_dst)

        for s2 in range(2):
            si = load_slab(w_in_d[li, :, 9 + s2], 4096)
            for c4 in range(4):
                c = s2 * 4 + c4
                k = proj_fm(si, 8, c4 * 128, hT, d_h)
                qk_norm_rope(k, qnw[:, 0:1], d_qnw, qsw[:, c, :], [d_big[16 + c]])
        si = load_slab(w_in_d[li, :, 11], 4096)
        for g in range(2):
            P.cp("dve", ksw[:, g, 0:128], khf[li][:, g * 128:(g + 1) * 128], [d_khf[li]], [d_ksw[g]])
        for g in range(2):
            k = proj_fm(si, 8, g * 128, hT, d_h)
            qk_norm_rope(k, knw[:, 0:1], d_knw, ksw[:, g, 128:128 + NT], [d_ksw[g]])
        for g in range(2):
            P.cp("dve", khf[li][:, g * 128:(g + 1) * 128], ksw[:, g, NT:NT + 128], [d_ksw[g]], [d_khf[li]])
        wv = slab_view(si, 8, 512)
        P.cp("dve", vsw[:, 0, :], vhf[li][:], [d_vhf[li]], [d_vsw])
        for half in range(2):
            k = nextpair()
            for t4 in range(4):
                tt_ = half * 4 + t4
                for kc in range(8):
                    P.mm(PSB[k][:, t4 * 128:(t4 + 1) * 128], hT[:, kc, tt_ * 128:(tt_ + 1) * 128], wv[:, kc, 256:384],
                         kc == 0, kc == 7, [d_h[kc], d_slab[si]], [d_ps[2 * k]])
            P.cp("act", vsw[:, 1 + half * 4:1 + (half + 1) * 4, :].rearrange("p a b -> p (a b)"), PSB[k][:, 0:512],
                 [d_ps[2 * k]], [d_vsw])
            if half == 1:
                P.cp("act", vhf[li][:], PSB[k][:, 384:512], [d_ps[2 * k]], [d_vhf[li]])
        attn = big[:, 8:16, :]
        f_s, is_ = FS[0], 0
        b_p, ibp = BS[0], 0
        b_pT, ibpT = BS[1], 1
        for qb in range(NQB):
            mk = swmask[:, 0:256] if qb == 0 else swmask[:, 256:512]
            for c in range(8):
                ko = nextpair()
                for e2 in range(2):
                    hq = 2 * c + e2
                    kvh = hq // 8
                    base = e2 * 64
                    k = nextpair()
                    P.mm(PSB[k][:, 0:256], qsw[base:base + 64, c, qb * 128:(qb + 1) * 128],
                         ksw[base:base + 64, kvh, qb * 128:qb * 128 + 256], True, True,
                         [d_big[16 + c], d_ksw[kvh]], [d_ps[2 * k]])
                    sc = f_s[:, e2 * 256:(e2 + 1) * 256]
                    P.stt(sc, PSB[k][:, 0:256], 0.125, mk, ALU.mult, ALU.add, [d_ps[2 * k], d_swmask], [d_fs[is_]])
                    m0 = sm[:, 0:1]
                    S.op("dve", lambda e, sc=sc, m0=m0: e.tensor_reduce(out=m0, in_=sc, axis=AX.X, op=ALU.max),
                         [d_fs[is_]], [d_sm])
                    P.tt(sm[:, 1:2], m0, sinks[:, hq:hq + 1], ALU.max, [d_sm, d_sinks], [d_sm])
                    P.ts(sm[:, 2:3], sm[:, 1:2], -1.0, None, ALU.mult, None, [d_sm], [d_sm])
                    P.act(sc, sc, AF.Exp, [d_fs[is_], d_sm], [d_fs[is_], d_sm], bias=sm[:, 2:3], scale=1.0, accum=sm[:, 3:4])
                    P.act(sm[:, 4:5], sinks[:, hq:hq + 1], AF.Exp, [d_sinks, d_sm], [d_sm], bias=sm[:, 2:3], scale=1.0)
                    P.tt(sm[:, 5:6], sm[:, 3:4], sm[:, 4:5], ALU.add, [d_sm], [d_sm])
                    S.op("dve", lambda e: e.reciprocal(out=sm[:, 6:7], in_=sm[:, 5:6]), [d_sm], [d_sm])
                    pn = b_p[:, e2 * 256:(e2 + 1) * 256]
                    P.ts(pn, sc, sm[:, 6:7], None, ALU.mult, None, [d_fs[is_], d_sm], [d_bs[ibp]])
                    pv = PSB[k][:, 512:1024].bitcast(BF16)
                    for kb2 in range(2):
                        P.tr(pv[:, kb2 * 128:(kb2 + 1) * 128], pn[:, kb2 * 128:(kb2 + 1) * 128], identb[:],
                             [d_bs[ibp], d_identb], [d_ps[2 * k + 1]])
                    pT = b_pT[:, e2 * 256:(e2 + 1) * 256]
                    P.cp("act", pT, pv[:, 0:256], [d_ps[2 * k + 1]], [d_bs[ibpT]])
                    for kb2 in range(2):
                        P.mm(PSB[ko][base:base + 64, 0:128], vsw[:, qb + kb2, kvh * 64:(kvh + 1) * 64],
                             pT[:, kb2 * 128:(kb2 + 1) * 128], kb2 == 0, kb2 == 1, [d_vsw, d_bs[ibpT]], [d_ps[2 * ko]])
                P.cp("act", attn[:, c, qb * 128:(qb + 1) * 128], PSB[ko][:, 0:128], [d_ps[2 * ko]], [d_big[8 + c]])

        f_ga, iga, f_m1, im1, f_gb, igb = FS[0], 0, FS[1], 1, FS[2], 2
        sl_dno = [load_slab(w_dno_d[li, :, s], 4096) for s in range(1)]
        for j in range(8):
            s2, j4 = j // 4, j % 4
            if j4 == 0:
                si_dno = load_slab(w_dno_d[li, :, s2], 4096) if j > 0 else sl_dno[0]
                si_ga = load_slab(w_in_d[li, :, 12 + s2], 4096)
            k = proj_fm(si_ga, 8, j4 * 128, hT, d_h)
            for hf in range(2):
                P.act(f_ga[:, hf * 512:(hf + 1) * 512], pb(k, hf), AF.Sigmoid, [d_ps[2 * k + hf]], [d_fs[iga]])
            k = proj_fm(si_dno, 8, j4 * 128, big, d_big, tile0=0)
            for hf in range(2):
                sl = slice(hf * 512, (hf + 1) * 512)
                P.tt(f_m1[:, sl], pb(k, hf), f_ga[:, sl], ALU.mult, [d_ps[2 * k + hf], d_fs[iga]], [d_fs[im1]])
            P.cp("pool", big[:, 16 + j, :], f_m1[:, 0:NT], [d_fs[im1]], [d_big[16 + j]])
        for j in range(8):
            s2, j4 = j // 4, j % 4
            if j4 == 0:
                si_swo = load_slab(w_swo_d[li, :, s2], 4096)
                si_gb = load_slab(w_in_d[li, :, 14 + s2], 4096)
            k = proj_fm(si_gb, 8, j4 * 128, hT, d_h)
            for hf in range(2):
                P.act(f_gb[:, hf * 512:(hf + 1) * 512], pb(k, hf), AF.Sigmoid, [d_ps[2 * k + hf]], [d_fs[igb]])
            k = proj_fm(si_swo, 8, j4 * 128, big, d_big, tile0=8)
            for hf in range(2):
                sl = slice(hf * 512, (hf + 1) * 512)
                P.tt(f_m1[:, sl], pb(k, hf), f_gb[:, sl], ALU.mult, [d_ps[2 * k + hf], d_fs[igb]], [d_fs[im1]])
            P.tt(big[:, 16 + j, :], big[:, 16 + j, :], f_m1[:, 0:NT], ALU.add, [d_big[16 + j], d_fs[im1]], [d_big[16 + j]])
        for j in range(8):
            s2, j4 = j // 4, j % 4
            if j4 == 0:
                si_o = load_slab(w_o_d[li, :, s2], 4096)
            k = proj_fm(si_o, 8, j4 * 128, big, d_big, tile0=16)
            for hf in range(2):
                sl = slice(hf * 512, (hf + 1) * 512)
                P.stt(xT[:, j, sl], pb(k, hf), mod[:, 16 + j:17 + j], xT[:, j, sl], ALU.mult, ALU.add,
                      [d_ps[2 * k + hf], d_mod, d_x[j]], [d_x[j]])
        if dbg and li == 0 and kb == 0:
            dbg_out(xT[:, 0, :], [d_x[0]])

        norm_mod(8, 24)
        f_P2, iP2, f_ac, iac = FS[5], 5, FS[6], 6
        for f in range(NF):
            if f % 2 == 0:
                si = load_slab(w_up_d[li, :, f // 2], 4096)
            fo = (f % 2) * 128
            k = proj_fm(si, 8, fo, hT, d_h)
            P.cp("pool", f_P2[:, 0:2], ffnh[li][:, f * 2:f * 2 + 2], [d_ffnh[li]], [d_fs[iP2]])
            for hf in range(2):
                P.cp("act", f_P2[:, 2 + hf * 512:2 + (hf + 1) * 512], pb(k, hf), [d_ps[2 * k + hf]], [d_fs[iP2]])
            P.cp("pool", ffnh[li][:, f * 2:f * 2 + 2], f_P2[:, NT:NT + 2], [d_fs[iP2]], [d_ffnh[li]])
            wc = lambda j: fconv[:, f * 3 + j:f * 3 + j + 1]
            P.ts(f_ac[:, 0:NT], f_P2[:, 2:2 + NT], wc(2), fconvb[:, f:f + 1], ALU.mult, ALU.add,
                 [d_fs[iP2], d_fconv, d_fconvb], [d_fs[iac]])
            for j in (1, 0):
                P.stt(f_ac[:, 0:NT], f_P2[:, j:j + NT], wc(j), f_ac[:, 0:NT], ALU.mult, ALU.add,
                      [d_fs[iP2], d_fconv, d_fs[iac]], [d_fs[iac]])
            P.act(f_ac[:, 0:NT], f_ac[:, 0:NT], AF.Silu, [d_fs[iac]], [d_fs[iac]])
            k = proj_fm(si, 8, 256 + fo, hT, d_h)
            for hf in range(2):
                sl = slice(hf * 512, (hf + 1) * 512)
                P.tt(big[:, f, sl], f_ac[:, sl], pb(k, hf), ALU.mult, [d_fs[iac], d_ps[2 * k + hf]], [d_big[f]])
        slabw[0] = 128
        for j in range(8):
            si = load_slab(w_dn_d[li, :, j], NF * 128)
            k = proj_fm(si, NF, 0, big, d_big, tile0=0)
            for hf in range(2):
                sl = slice(hf * 512, (hf + 1) * 512)
                P.stt(xT[:, j, sl], pb(k, hf), mod[:, 40 + j:41 + j], xT[:, j, sl], ALU.mult, ALU.add,
                      [d_ps[2 * k + hf], d_mod, d_x[j]], [d_x[j]])
        slabw[0] = 512
        for c in range(8):
            t = S.dma("sp", lambda e, c=c: e.dma_start(out=yT_d[:, c, t0:t0 + NT], in_=xT[:, c, :]), R=[d_x[c]])
            out_toks.append(t)

    for li in range(L):
        for kb in range(nblk):
            emit_pass(li, kb)
    for l in range(L):
        out_toks.append(S.dma("sp", lambda e, l=l: e.dma_start(out=S_out_d[l], in_=Sst[l][:]), R=d_Sst[l]))
        out_toks.append(S.dma("sp", lambda e, l=l: e.dma_start(out=convh_out_d[l], in_=convh[l][:]), R=[d_convh[l]]))
        out_toks.append(S.dma("sp", lambda e, l=l: e.dma_start(out=ffnh_out_d[l], in_=ffnh[l][:]), R=[d_ffnh[l]]))
        out_toks.append(S.dma("sp", lambda e, l=l: e.dma_start(out=kh_out_d[l], in_=khf[l][:]), R=[d_khf[l]]))
        out_toks.append(S.dma("sp", lambda e, l=l: e.dma_start(out=vh_out_d[l], in_=vhf[l][:]), R=[d_vhf[l]]))
    S.finish_wait("sp", out_toks)
    S.build(nc, P.st)
    P.st.close()
    return nc, S


def _slabify(W, cw):
    K, N = W.shape
    return np.ascontiguousarray(W.reshape(K // 128, 128, N // cw, cw).transpose(1, 2, 0, 3)).reshape(128, N // cw, (K // 128) * cw)


def _w_in_index():
    idx = -np.ones((16, 512), dtype=np.int64)
    ar = np.arange
    for h in range(8):
        idx[h] = np.concatenate([h * 128 + ar(128), 1024 + h * 128 + ar(128), 2048 + h * 128 + ar(128), 3072 + h * 128 + ar(128)])
    idx[8, 0:16] = 4096 + ar(16)
    idx[9] = 4112 + ar(512)
    idx[10] = 4112 + 512 + ar(512)
    k0 = 5136 + ar(64)
    k1 = 5136 + 64 + ar(64)
    idx[11, 0:256] = np.concatenate([k0, k0, k1, k1])
    idx[11, 256:384] = 5264 + ar(128)
    idx[12] = 5392 + ar(512)
    idx[13] = 5392 + 512 + ar(512)
    idx[14] = 6416 + ar(512)
    idx[15] = 6416 + 512 + ar(512)
    return idx.reshape(-1)


def _consts(first_block):
    c = {}
    c["ident"] = np.eye(128, dtype=np.float32)
    c["onesm"] = np.ones((128, 128), np.float32)
    bd = np.zeros((128, 128), np.float32)
    bd[:64, :64] = 1
    bd[64:, 64:] = 1
    c["bd64"] = bd
    rm = np.zeros((128, 128), np.float32)
    for g in range(2):
        for d in range(8):
            rm[g * 64 + d + 8, g * 64 + d] = -1.0
            rm[g * 64 + d, g * 64 + d + 8] = 1.0
    c["rm"] = rm
    half = 8
    inv = np.power(np.float32(500000.0), -np.arange(half, dtype=np.float32) / np.float32(half)).astype(np.float32)
    invf = np.zeros((1, 128), np.float32)
    for p in range(128):
        d = p % 64
        if d < 16:
            invf[0, p] = inv[d % 8]
    c["invf"] = invf
    sel = np.zeros((8, 8, 128), np.float32)
    for h in range(8):
        sel[h, h, :] = 1
    c["sel"] = sel.reshape(8, 1024)
    rs = np.ones((8, NT), np.float32)
    rs[:, ::64] = 0
    c["resetm"] = rs
    j = np.arange(64)[:, None]
    i = np.arange(64)[None, :]
    mi = np.zeros((128, 64), np.float32)
    ms = np.zeros((128, 64), np.float32)
    it = np.zeros((128, 64), np.float32)
    mi[:64] = (i >= j)
    ms[:64] = (i > j)
    it[:64] = (i == j)
    c["mincl"] = np.tile(mi, (1, NCH))
    c["mstrict"] = np.tile(ms, (1, NCH))
    c["itile"] = np.tile(it, (1, NCH))
    qi = np.arange(128)[:, None]
    kj = np.arange(256)[None, :]
    valid = (kj > qi) & (kj <= qi + 128)
    m1 = np.where(valid, 0.0, -30000.0).astype(np.float32)
    m0 = np.where(valid & (kj >= 128), 0.0, -30000.0).astype(np.float32)
    c["swmask"] = np.concatenate([m0 if first_block else m1, m1], axis=1)
    return c


def _layer_weights(inp, l):
    f = np.float32
    d = {}
    d["w_ada"] = _slabify(np.asarray(inp["w_ada"][l], f), 512)
    d["b_ada"] = np.ascontiguousarray(np.asarray(inp["b_ada"][l], f).reshape(48, 128).T)
    d["nmix"] = np.ascontiguousarray(np.asarray(inp["norm_mix"][l], f).reshape(8, 128).T)
    d["nffn"] = np.ascontiguousarray(np.asarray(inp["norm_ffn"][l], f).reshape(8, 128).T)
    w = np.asarray(inp["w_in"][l], f)
    idx = _w_in_index()
    wg = np.zeros((1024, idx.shape[0]), f)
    ok = idx >= 0
    wg[:, ok] = w[:, idx[ok]]
    d["w_in"] = _slabify(wg, 512)
    dc = np.asarray(inp["dn_conv"][l], f)
    d["dnconv"] = np.ascontiguousarray(dc.reshape(4, 24, 128).transpose(2, 1, 0)).reshape(128, 96)
    d["alog"] = np.asarray(inp["dn_a_log"][l], f).reshape(8, 1)
    d["dtb"] = np.asarray(inp["dn_dt_bias"][l], f).reshape(8, 1)
    d["dnnorm"] = np.asarray(inp["dn_norm"][l], f).reshape(128, 1)
    d["w_dno"] = _slabify(np.asarray(inp["w_dn_out"][l], f), 512)
    d["qnw"] = np.tile(np.asarray(inp["swa_q_norm"][l], f), 2).reshape(128, 1)
    d["knw"] = np.tile(np.asarray(inp["swa_k_norm"][l], f), 2).reshape(128, 1)
    d["sinks"] = np.ascontiguousarray(np.broadcast_to(np.asarray(inp["swa_sinks"][l], f)[None, :], (128, 16)))
    d["w_swo"] = _slabify(np.asarray(inp["w_swa_out"][l], f), 512)
    d["w_o"] = _slabify(np.asarray(inp["w_o"][l], f), 512)
    wu = np.asarray(inp["w_up"][l], f)
    ar = np.arange
    uidx = np.concatenate([np.concatenate([s * 256 + ar(256), DFF + s * 256 + ar(256)]) for s in range(11)])
    d["w_up"] = _slabify(np.ascontiguousarray(wu[:, uidx]), 512)
    fc = np.asarray(inp["ffn_conv"][l], f)
    d["fconv"] = np.ascontiguousarray(fc.reshape(3, NF, 128).transpose(2, 1, 0)).reshape(128, NF * 3)
    d["fconvb"] = np.ascontiguousarray(np.asarray(inp["ffn_conv_b"][l], f).reshape(NF, 128).T)
    d["w_dn"] = _slabify(np.asarray(inp["w_down"][l], f), 128)
    return d


_WKEYS = ["w_ada", "b_ada", "nmix", "nffn", "w_in", "dnconv", "alog", "dtb", "dnnorm", "w_dno", "qnw", "knw",
          "sinks", "w_swo", "w_o", "w_up", "fconv", "fconvb", "w_dn"]
_SKEYS = [("S", (128, 1024)), ("convh", (128, 72)), ("ffnh", (128, NF * 2)), ("kh", (128, 256)), ("vh", (128, 128))]

_CACHE = {}


def _get_prog(L, nblk, dbg=False):
    key = (L, nblk, dbg)
    if key not in _CACHE:
        _CACHE[key] = build_program(L, nblk, dbg)
    return _CACHE[key]


def kernel(**inputs):
    x = np.asarray(inputs["x"], np.float32)
    c = np.asarray(inputs["c"], np.float32)
    pos = np.asarray(inputs["positions"], np.int32)
    B, T, _ = x.shape
    nblk_seq = T // NT
    nc, _ = _get_prog(1, 1)
    lw = [_layer_weights(inputs, l) for l in range(4)]
    consts = {True: _consts(True), False: _consts(False)}
    cur = [np.ascontiguousarray(x[b].T.reshape(8, 128, T).transpose(1, 0, 2)) for b in range(B)]
    cTs = [np.ascontiguousarray(c[b].reshape(8, 128).T) for b in range(B)]
    for l in range(4):
        states = [{k: np.zeros((1,) + shp, np.float32) for k, shp in _SKEYS} for b in range(B)]
        for kb in range(nblk_seq):
            in_maps = []
            for core in range(8):
                b = core % B
                m = {"xT": np.ascontiguousarray(cur[b][:, :, kb * NT:(kb + 1) * NT]), "cT": cTs[b],
                     "pos": np.ascontiguousarray(pos[b:b + 1, kb * NT:(kb + 1) * NT])}
                for k in _WKEYS:
                    m[k] = lw[l][k][None]
                m.update(consts[kb == 0])
                for k, _ in _SKEYS:
                    m[k + "_in"] = states[b][k]
                in_maps.append(m)
            res = run_bass_kernel_spmd(nc, in_maps, core_ids=list(range(8)))
            for b in range(B):
                r = res.results[b]
                cur[b][:, :, kb * NT:(kb + 1) * NT] = r["yT"]
                for k, _ in _SKEYS:
                    states[b][k] = np.asarray(r[k + "_out"], np.float32)
    out = np.stack([cur[b].transpose(1, 0, 2).reshape(1024, T).T for b in range(B)], axis=0)
    return np.ascontiguousarray(out.astype(np.float32))
```
